# Optimizing a Trainium2 kernel written in Bass

```python
import math
import jax, jax.numpy as jnp
from jax import lax
import numpy as np

D_MODEL = 2048
BATCH = 4
SEQ = 4096
DEPTH = 2

N_MIXERS = 2
N_MLA_LAYERS = (DEPTH + 1) // 2
N_GDN_LAYERS = DEPTH // 2
RMS_EPS = 1e-6

MLA_HEADS = 16
MLA_Q_RANK = 512
MLA_KV_RANK = 512
MLA_NOPE_DIM = 128
MLA_ROPE_DIM = 64
MLA_V_DIM = 128
MLA_QK_DIM = MLA_NOPE_DIM + MLA_ROPE_DIM
MLA_IN_DIM = MLA_Q_RANK + MLA_KV_RANK + MLA_ROPE_DIM
ROPE_THETA = 10000.0
Q_BLOCK = 128

GDN_QK_HEADS = 16
GDN_V_HEADS = 32
GDN_HEAD_DIM_K = 128
GDN_HEAD_DIM_V = 128
GDN_KEY_DIM = GDN_QK_HEADS * GDN_HEAD_DIM_K
GDN_VALUE_DIM = GDN_V_HEADS * GDN_HEAD_DIM_V
GDN_CONV_DIM = 2 * GDN_KEY_DIM + GDN_VALUE_DIM
GDN_IN_DIM = GDN_CONV_DIM + GDN_VALUE_DIM + 2 * GDN_V_HEADS
GDN_CONV = 4
GDN_CHUNK = 64

D_FF = 5632
FFN_CONV = 3

kernel_name = "hybrid_mla_gdn_convffn"


def rmsnorm(x, w):
    xf = x.astype(jnp.float32)
    y = xf * lax.rsqrt(jnp.mean(xf * xf, axis=-1, keepdims=True) + RMS_EPS)
    return (y * w.astype(jnp.float32)).astype(x.dtype)


def l2norm(t):
    return t * lax.rsqrt(jnp.sum(t * t, axis=-1, keepdims=True) + 1e-6)


def causal_dwconv(x, w):
    width, ch = w.shape
    return lax.conv_general_dilated(
        x, w[:, None, :].astype(x.dtype), window_strides=(1,),
        padding=[(width - 1, 0)], dimension_numbers=("NWC", "WIO", "NWC"),
        feature_group_count=ch)


def rope_tables(positions, dim):
    inv_freq = ROPE_THETA ** (-jnp.arange(0, dim, 2, dtype=jnp.float32) / dim)
    ang = positions.astype(jnp.float32)[..., None] * inv_freq
    return jnp.cos(ang), jnp.sin(ang)


def apply_rope(t, cos, sin):
    t1, t2 = jnp.split(t.astype(jnp.float32), 2, axis=-1)
    return jnp.concatenate([t1 * cos - t2 * sin, t2 * cos + t1 * sin], axis=-1).astype(t.dtype)


def mla_mixer(h, positions, w_in, q_norm, kv_norm, w_uq, w_ukv, w_o):
    B, S, _ = h.shape
    proj = h @ w_in
    c_q, c_kv, k_rope = jnp.split(proj, [MLA_Q_RANK, MLA_Q_RANK + MLA_KV_RANK], axis=-1)
    q = (rmsnorm(c_q, q_norm) @ w_uq).reshape(B, S, MLA_HEADS, MLA_QK_DIM)
    q_nope, q_rope = q[..., :MLA_NOPE_DIM], q[..., MLA_NOPE_DIM:]
    kv = (rmsnorm(c_kv, kv_norm) @ w_ukv).reshape(B, S, MLA_HEADS, MLA_NOPE_DIM + MLA_V_DIM)
    k_nope, v = kv[..., :MLA_NOPE_DIM], kv[..., MLA_NOPE_DIM:]
    cos, sin = rope_tables(positions, MLA_ROPE_DIM)
    q_rope = apply_rope(q_rope, cos[:, :, None, :], sin[:, :, None, :])
    k_rope = apply_rope(k_rope, cos, sin)
    scale = MLA_QK_DIM ** -0.5
    n_blk = S // Q_BLOCK
    qn = jnp.moveaxis(q_nope.reshape(B, n_blk, Q_BLOCK, MLA_HEADS, MLA_NOPE_DIM), 1, 0)
    qr = jnp.moveaxis(q_rope.reshape(B, n_blk, Q_BLOCK, MLA_HEADS, MLA_ROPE_DIM), 1, 0)
    key_idx = jnp.arange(S)

    def one_block(args):
        blk, qn_b, qr_b = args
        s = (jnp.einsum('bqhd,bkhd->bhqk', qn_b, k_nope, preferred_element_type=jnp.float32)
             + jnp.einsum('bqhr,bkr->bhqk', qr_b, k_rope, preferred_element_type=jnp.float32)) * scale
        q_idx = blk * Q_BLOCK + jnp.arange(Q_BLOCK)
        mask = key_idx[None, :] <= q_idx[:, None]
        s = jnp.where(mask, s, jnp.finfo(jnp.float32).min)
        p = jax.nn.softmax(s, axis=-1).astype(v.dtype)
        return jnp.einsum('bhqk,bkhd->bqhd', p, v)

    o = lax.map(one_block, (jnp.arange(n_blk), qn, qr))
    o = jnp.moveaxis(o, 0, 1).reshape(B, S, MLA_HEADS * MLA_V_DIM)
    return o @ w_o


def chunk_gated_delta_rule(q, k, v, g, beta):
    B, S, H, DK = q.shape
    DV = v.shape[-1]
    C = GDN_CHUNK
    N = S // C

    def to_chunks(t):
        t = t.reshape((B, N, C) + t.shape[2:])
        return jnp.moveaxis(t, 3, 2)

    q, k, v, g, beta = (to_chunks(t) for t in (q, k, v, g, beta))
    g_cum = jnp.cumsum(g, axis=-1)
    tril = jnp.tril(jnp.ones((C, C), dtype=bool))
    tril_strict = jnp.tril(jnp.ones((C, C), dtype=bool), -1)
    diff = g_cum[..., :, None] - g_cum[..., None, :]
    decay = jnp.exp(jnp.where(tril, diff, -jnp.inf))
    k_beta = k * beta[..., None]
    v_beta = v * beta[..., None]
    L = jnp.where(tril_strict, jnp.einsum('bnhcd,bnhed->bnhce', k_beta, k) * decay, 0.0)
    eye = jnp.eye(C, dtype=jnp.float32)
    T = lax.linalg.triangular_solve(eye + L, jnp.broadcast_to(eye, L.shape),
                                    left_side=True, lower=True)
    u = T @ v_beta
    w = T @ (k_beta * jnp.exp(g_cum)[..., None])
    attn_intra = jnp.einsum('bnhcd,bnhed->bnhce', q, k) * decay
    q_dec = q * jnp.exp(g_cum)[..., None]
    k_dec = k * jnp.exp(g_cum[..., -1:] - g_cum)[..., None]
    g_tot = jnp.exp(g_cum[..., -1])

    def step(state, inp):
        w_c, u_c, q_c, k_c, a_c, gt = inp
        v_new = u_c - jnp.einsum('bhcd,bhdv->bhcv', w_c, state)
        o_c = jnp.einsum('bhcd,bhdv->bhcv', q_c, state) + jnp.einsum('bhce,bhev->bhcv', a_c, v_new)
        state = state * gt[..., None, None] + jnp.einsum('bhcd,bhcv->bhdv', k_c, v_new)
        return state, o_c

    xs = tuple(jnp.moveaxis(t, 1, 0) for t in (w, u, q_dec, k_dec, attn_intra, g_tot))
    state0 = jnp.zeros((B, H, DK, DV), jnp.float32)
    _, o = lax.scan(step, state0, xs)
    return o.transpose(1, 0, 3, 2, 4).reshape(B, S, H, DV)


def gdn_mixer(h, w_in, conv_w, a_log, dt_bias, out_norm, w_o):
    B, S, _ = h.shape
    proj = h @ w_in
    qkv, z, b, a = jnp.split(proj, [GDN_CONV_DIM, GDN_CONV_DIM + GDN_VALUE_DIM,
                                    GDN_CONV_DIM + GDN_VALUE_DIM + GDN_V_HEADS], axis=-1)
    qkv = jax.nn.silu(causal_dwconv(qkv, conv_w)).astype(jnp.float32)
    q, k, v = jnp.split(qkv, [GDN_KEY_DIM, 2 * GDN_KEY_DIM], axis=-1)
    q = l2norm(q.reshape(B, S, GDN_QK_HEADS, GDN_HEAD_DIM_K)) * (GDN_HEAD_DIM_K ** -0.5)
    k = l2norm(k.reshape(B, S, GDN_QK_HEADS, GDN_HEAD_DIM_K))
    v = v.reshape(B, S, GDN_V_HEADS, GDN_HEAD_DIM_V)
    rep = GDN_V_HEADS // GDN_QK_HEADS
    q = jnp.repeat(q, rep, axis=2)
    k = jnp.repeat(k, rep, axis=2)
    beta = jax.nn.sigmoid(b.astype(jnp.float32))
    g = -jnp.exp(a_log.astype(jnp.float32)) * jax.nn.softplus(
        a.astype(jnp.float32) + dt_bias.astype(jnp.float32))
    o = chunk_gated_delta_rule(q, k, v, g, beta)
    zf = z.astype(jnp.float32).reshape(B, S, GDN_V_HEADS, GDN_HEAD_DIM_V)
    o = rmsnorm(o, out_norm) * jax.nn.silu(zf)
    return o.reshape(B, S, GDN_VALUE_DIM).astype(h.dtype) @ w_o


def conv_ffn(h, w_up, conv_w, conv_b, w_down):
    u = causal_dwconv(h @ w_up, conv_w) + conv_b
    gate, up = jnp.split(u, 2, axis=-1)
    return (jax.nn.silu(gate) * up) @ w_down


def setup_inputs(seed: int = 0) -> dict:
    key = jax.random.key(seed)
    ks = iter(jax.random.split(key, 32))

    def nrm(shape, scale):
        return jax.random.normal(next(ks), shape, jnp.float32) * scale

    def gain(shape):
        return 1.0 + nrm(shape, 0.02)

    D = D_MODEL
    x = jax.random.normal(next(ks), (BATCH, SEQ, D), jnp.float32)
    offs = jax.random.randint(next(ks), (BATCH, 1), 0, 1024, dtype=jnp.int32)
    positions = (jnp.arange(SEQ, dtype=jnp.int32)[None, :] + offs).astype(jnp.int32)
    nm, ng = N_MLA_LAYERS, N_GDN_LAYERS
    A = jax.random.uniform(next(ks), (ng, GDN_V_HEADS), jnp.float32, 1.0, 16.0)
    dt = jnp.exp(jax.random.uniform(next(ks), (ng, GDN_V_HEADS), jnp.float32,
                                    math.log(1e-3), math.log(0.1)))
    return {
        "x": x,
        "positions": positions,
        "mla_norm": gain((nm, D)),
        "mla_w_in": nrm((nm, D, MLA_IN_DIM), D ** -0.5),
        "mla_q_norm": gain((nm, MLA_Q_RANK)),
        "mla_kv_norm": gain((nm, MLA_KV_RANK)),
        "mla_w_uq": nrm((nm, MLA_Q_RANK, MLA_HEADS * MLA_QK_DIM), MLA_Q_RANK ** -0.5),
        "mla_w_ukv": nrm((nm, MLA_KV_RANK, MLA_HEADS * (MLA_NOPE_DIM + MLA_V_DIM)), MLA_KV_RANK ** -0.5),
        "mla_w_o": nrm((nm, MLA_HEADS * MLA_V_DIM, D), (MLA_HEADS * MLA_V_DIM) ** -0.5),
        "gdn_norm": gain((ng, D)),
        "gdn_w_in": nrm((ng, D, GDN_IN_DIM), D ** -0.5),
        "gdn_conv_w": nrm((ng, GDN_CONV, GDN_CONV_DIM), GDN_CONV ** -0.5),
        "gdn_a_log": jnp.log(A),
        "gdn_dt_bias": dt + jnp.log(-jnp.expm1(-dt)),
        "gdn_out_norm": gain((ng, GDN_HEAD_DIM_V)),
        "gdn_w_o": nrm((ng, GDN_VALUE_DIM, D), GDN_VALUE_DIM ** -0.5),
        "ffn_norm": gain((DEPTH, D)),
        "ffn_w_up": nrm((DEPTH, D, 2 * D_FF), D ** -0.5),
        "ffn_conv_w": nrm((DEPTH, FFN_CONV, 2 * D_FF), FFN_CONV ** -0.5),
        "ffn_conv_b": nrm((DEPTH, 2 * D_FF), 0.01),
        "ffn_w_down": nrm((DEPTH, D_FF, D), D_FF ** -0.5),
        "final_norm": gain((D,)),
    }


def reference(x, positions, mla_norm, mla_w_in, mla_q_norm, mla_kv_norm, mla_w_uq, mla_w_ukv,
              mla_w_o, gdn_norm, gdn_w_in, gdn_conv_w, gdn_a_log, gdn_dt_bias, gdn_out_norm,
              gdn_w_o, ffn_norm, ffn_w_up, ffn_conv_w, ffn_conv_b, ffn_w_down, final_norm):
    h = x
    for i in range(DEPTH):
        j = i // N_MIXERS
        if i % N_MIXERS == 0:
            h = h + mla_mixer(rmsnorm(h, mla_norm[j]), positions, mla_w_in[j], mla_q_norm[j],
                              mla_kv_norm[j], mla_w_uq[j], mla_w_ukv[j], mla_w_o[j])
        else:
            h = h + gdn_mixer(rmsnorm(h, gdn_norm[j]), gdn_w_in[j], gdn_conv_w[j], gdn_a_log[j],
                              gdn_dt_bias[j], gdn_out_norm[j], gdn_w_o[j])
        h = h + conv_ffn(rmsnorm(h, ffn_norm[i]), ffn_w_up[i], ffn_conv_w[i], ffn_conv_b[i],
                         ffn_w_down[i])
    return rmsnorm(h, final_norm)
```

```python
from contextlib import ExitStack
import numpy as np
import ml_dtypes
import concourse.bass as bass
import concourse.mybir as mybir
from concourse.bass_utils import run_bass_kernel_spmd

F32 = mybir.dt.float32
BF16 = mybir.dt.bfloat16
I32 = mybir.dt.int32
AF = mybir.ActivationFunctionType
ALU = mybir.AluOpType
NPBF = ml_dtypes.bfloat16

ENG = ('pe', 'dve', 'act', 'pool', 'sp')


class AS:
    def __init__(s, nc, es):
        s.nc = nc
        s.es = es
        s.e = dict(pe=nc.tensor, dve=nc.vector, act=nc.scalar, pool=nc.gpsimd, sp=nc.sync)
        s.semh = {k: es.enter_context(nc.semaphore("S_" + k)) for k in ENG}
        s.cnt = {k: 0 for k in ENG}
        s.known = {k: {} for k in ENG}
        s.snaps = {k: [None] for k in ENG}
        s.st = {}
        s.dcnt = {}
        s.nd = 0

    def _deps(s, en, R, W):
        need = {}

        def add(ev, same_ok):
            if ev is None:
                return
            sk, v = ev
            if same_ok and sk == en:
                return
            if need.get(sk, 0) < v:
                need[sk] = v
        for k in R:
            st = s.st.get(k)
            if st:
                add(st[0], False)
        for k in W:
            st = s.st.get(k)
            if st:
                add(st[0], True)
                for sk, v in st[1].items():
                    add((sk, v), True)
        kn = s.known[en]
        return [(sk, v) for sk, v in need.items() if kn.get(sk, 0) < v]

    def _apply(s, en, sk, v):
        kn = s.known[en]
        if kn.get(sk, 0) < v:
            kn[sk] = v
        if sk in s.cnt:
            snap = s.snaps[sk][v]
            for e2, v2 in zip(ENG, snap):
                if kn.get(e2, 0) < v2:
                    kn[e2] = v2

    def _record(s, ev, R, W):
        sk, v = ev
        for k in R:
            st = s.st.setdefault(k, [None, {}])
            if st[1].get(sk, 0) < v:
                st[1][sk] = v
        for k in W:
            s.st[k] = [ev, {}]

    def _emit(s, en, build, waits):
        eng = s.e[en]
        for sk, v in waits:
            assert sk not in s.cnt or v <= s.cnt[sk], ("dep on future inc", en, sk, v)
        for sk, v in waits[1:]:
            eng.wait_ge(s.semh[sk], v)
        ins = build(eng)
        if waits:
            ins._wait_ge(s.semh[waits[0][0]], waits[0][1])
        for sk, v in waits:
            s._apply(en, sk, v)
        return ins

    def op(s, en, build, R=(), W=(), inc=True):
        ins = s._emit(en, build, s._deps(en, R, W))
        if inc:
            s.cnt[en] += 1
            ins.then_inc(s.semh[en], 1)
            s.snaps[en].append(tuple(s.known[en].get(e, 0) for e in ENG))
            ev = (en, s.cnt[en])
        else:
            ev = (en, s.cnt[en] + 1)
        s._record(ev, R, W)
        return ins

    def dma(s, q, out, in_, R=(), W=(), sk=None):
        sk = ('d', sk)
        if sk not in s.semh:
            s.semh[sk] = s.es.enter_context(s.nc.semaphore("D%d" % s.nd))
            s.nd += 1
            s.dcnt[sk] = 0
        ins = s._emit(q, lambda eng: eng.dma_start(out=out, in_=in_), s._deps(q, R, W))
        s.dcnt[sk] += 16
        ins.then_inc(s.semh[sk], 16)
        s._record((sk, s.dcnt[sk]), R, W)
        return ins

    def finish(s, q='sp'):
        eng = s.e[q]
        for sk, v in s.dcnt.items():
            if s.known[q].get(sk, 0) < v:
                eng.wait_ge(s.semh[sk], v)


class KB:
    def __init__(s):
        s.nc = bass.Bass("TRN2", target_bir_lowering=False)
        s.es = ExitStack()
        s.a = AS(s.nc, s.es)
        s.n = 0

    def din(s, name, shape, dt):
        return s.nc.dram_tensor(name, list(shape), dt, kind="ExternalInput").ap()

    def dout(s, name, shape, dt):
        return s.nc.dram_tensor(name, list(shape), dt, kind="ExternalOutput").ap()

    def sb(s, name, shape, dt):
        return s.es.enter_context(s.nc.sbuf_tensor(name, list(shape), dt))

    def ps(s, name, shape, dt=F32):
        return s.es.enter_context(s.nc.psum_tensor(name, list(shape), dt))

    def close(s):
        s.a.finish('sp')
        s.es.close()
        return s.nc


def _mm(a, en_unused, ps, lhsT, rhs, start, stop, R, W, inc=None, **kw):
    return a.op('pe', lambda e: e.matmul(ps, lhsT=lhsT, rhs=rhs, start=start, stop=stop, **kw),
                R=R, W=W, inc=(stop if inc is None else inc))


def rmsnorm_fm(kb, src, srck, nchunk, N, gain, onesb, epsb, ps_ssq, sq, rstd, dst, dstk, D,
               rows=128):
    a = kb.a
    for c in range(nchunk):
        b = c % len(sq)
        a.op('act', lambda e, c=c, b=b: e.activation(out=sq[b][0:rows, 0:N], in_=src[c], func=AF.Square),
             R=[srck[c]], W=[('sq', b)])
        a.op('pe', lambda e, c=c, b=b: e.matmul(ps_ssq[0:rows, 0:N], lhsT=onesb[0:rows, 0:rows], rhs=sq[b][0:rows, 0:N],
                                           start=(c == 0), stop=(c == nchunk - 1)),
             R=[('sq', b)], W=['ps_ssq'], inc=True)
    a.op('act', lambda e: e.activation(out=rstd[0:rows, 0:N], in_=ps_ssq[0:rows, 0:N], func=AF.Sqrt,
                                       bias=epsb[0:rows, :], scale=1.0 / D), R=['ps_ssq'], W=['rstd'])
    a.op('dve', lambda e: e.reciprocal(out=rstd[0:rows, 0:N], in_=rstd[0:rows, 0:N]), R=['rstd'], W=['rstd'])
    for c in range(nchunk):
        a.op('dve', lambda e, c=c: e.scalar_tensor_tensor(out=dst[c], in0=src[c], scalar=gain[0:rows, c:c + 1],
                                                        in1=rstd[0:rows, 0:N], op0=ALU.mult, op1=ALU.mult),
             R=[srck[c], 'rstd', 'consts'], W=[dstk[c]])


DM = 2048
DFF = 5632
NFF = DFF // 128
HL = 2


def build_of(T, KO, final):
    kb = KB()
    a = kb.a
    nc = kb.nc
    KC = KO // 128
    hT = kb.din("hT", [DM, HL + T], F32)
    oT = kb.din("oT", [KO, HL + T], BF16)
    w_o = kb.din("w_o", [16, 128, KC, 128], F32)
    w_up = kb.din("w_up", [2 * NFF, 128, 16, 128], F32)
    w_dn = kb.din("w_dn", [16, 128, NFF, 128], F32)
    cst = kb.din("cst", [128, 16 + 16 + 2 * NFF * 4], F32)
    outT = kb.dout("outT", [DM, T], F32)

    NT = 512
    big = kb.sb("big", [128, NFF * NT], BF16)
    h1 = kb.sb("h1", [128, 16 * NT], F32)
    xn = kb.sb("xn", [128, 16 * NT], BF16)
    sq = [kb.sb("sq%d" % i, [128, NT], BF16) for i in range(2)]
    rstd = kb.sb("rstd", [128, NT], F32)
    cs = kb.sb("cs", [128, 16 + 16 + 2 * NFF * 4], F32)
    onesb = kb.sb("onesb", [128, 128], BF16)
    epsb = kb.sb("epsb", [128, 1], F32)
    carry = kb.sb("carry", [128, 2 * NFF * 2], F32)
    uext = [kb.sb("uext%d" % i, [128, HL + NT], F32) for i in range(4)]
    t1 = [kb.sb("t1_%d" % i, [128, NT], F32) for i in range(4)]
    t2 = [kb.sb("t2_%d" % i, [128, NT], F32) for i in range(4)]
    wo_b = [kb.sb("wo%d" % i, [128, KC * 128], BF16) for i in range(2)]
    wu_b = [kb.sb("wu%d" % i, [128, 16 * 128], BF16) for i in range(4)]
    wd_b = [kb.sb("wd%d" % i, [128, NFF * 128], BF16) for i in range(2)]
    ob = [kb.sb("ob%d" % i, [128, NT], F32) for i in range(2)]
    ps_a = [kb.ps("psa%d" % i, [128, NT]) for i in range(2)]
    ps_u = [kb.ps("psu%d" % i, [128, NT]) for i in range(4)]
    ps_ssq = kb.ps("ps_ssq", [128, NT])

    a.op('pool', lambda e: e.memset(onesb[:], 1.0), W=['consts'])
    a.op('pool', lambda e: e.memset(epsb[:], 1e-6), W=['consts'])
    a.dma('sp', cs[:], cst[:, :], W=['consts'], sk='cs')
    GF, GN, CV = 0, 16, 32

    cnt = dict(wo=0, wu=0, wd=0, ob=0, ue=0)
    tiles = [(0, HL, True)] + [(HL + i * NT, NT, False) for i in range(T // NT)]
    for (c0, N, halo) in tiles:
        for kc in range(KC):
            a.dma('sp', big[:, kc * NT:kc * NT + N], oT[kc * 128:(kc + 1) * 128, c0:c0 + N],
                  W=[('big', kc)], sk=('big', kc))
        for m in range(16):
            a.dma('sp', h1[:, m * NT:m * NT + N], hT[m * 128:(m + 1) * 128, c0:c0 + N],
                  W=[('h1', m)], sk=('h1', m))
        for m in range(16):
            wb = cnt['wo'] % 2
            cnt['wo'] += 1
            a.dma('pool', wo_b[wb][:], w_o[m].rearrange("p k j -> p (k j)"), W=[('wo', wb)], sk=('wo', wb))
            pb = m % 2
            for kc in range(KC):
                a.op('pe', lambda e, kc=kc, wb=wb, pb=pb: e.matmul(
                    ps_a[pb][:, 0:N], lhsT=wo_b[wb][:, kc * 128:(kc + 1) * 128], rhs=big[:, kc * NT:kc * NT + N],
                    start=(kc == 0), stop=(kc == KC - 1)),
                    R=[('wo', wb), ('big', kc)], W=[('psa', pb)], inc=(kc == KC - 1))
            a.op('dve', lambda e, m=m, pb=pb: e.tensor_tensor(out=h1[:, m * NT:m * NT + N], in0=h1[:, m * NT:m * NT + N],
                                                          in1=ps_a[pb][:, 0:N], op=ALU.add),
                 R=[('psa', pb), ('h1', m)], W=[('h1', m)])
        rmsnorm_fm(kb, [h1[:, m * NT:m * NT + N] for m in range(16)], [('h1', m) for m in range(16)], 16, N,
                   cs[:, GF:GF + 16], onesb, epsb, ps_ssq, sq, rstd,
                   [xn[:, m * NT:m * NT + N] for m in range(16)], [('xn', m) for m in range(16)], DM)
        for j in range(NFF):
            res = []
            for gu in range(2):
                ch = gu * NFF + j
                wb = cnt['wu'] % 4
                cnt['wu'] += 1
                a.dma('pool', wu_b[wb][:], w_up[ch].rearrange("p k j -> p (k j)"), W=[('wu', wb)], sk=('wu', wb))
                pb = wb
                for kc in range(16):
                    a.op('pe', lambda e, kc=kc, wb=wb, pb=pb: e.matmul(
                        ps_u[pb][:, 0:N], lhsT=wu_b[wb][:, kc * 128:(kc + 1) * 128], rhs=xn[:, kc * NT:kc * NT + N],
                        start=(kc == 0), stop=(kc == 15)),
                        R=[('wu', wb), ('xn', kc)], W=[('psu', pb)], inc=(kc == 15))
                ub = cnt['ue'] % 4
                cnt['ue'] += 1
                ue = uext[ub]
                cc = cs[:, CV + ch * 4:CV + ch * 4 + 4]
                cr = carry[:, ch * 2:ch * 2 + 2]
                if halo:
                    a.op('act', lambda e, cr=cr, pb=pb: e.copy(out=cr, in_=ps_u[pb][:, 0:HL]),
                         R=[('psu', pb)], W=[('carry', ch)])
                    continue
                a.op('act', lambda e, ue=ue, pb=pb: e.copy(out=ue[:, HL:HL + N], in_=ps_u[pb][:, 0:N]),
                     R=[('psu', pb)], W=[('ue', ub)])
                a.op('act', lambda e, ue=ue, cr=cr: e.copy(out=ue[:, 0:HL], in_=cr),
                     R=[('carry', ch)], W=[('ue', ub)])
                a.op('act', lambda e, ue=ue, cr=cr: e.copy(out=cr, in_=ue[:, N:N + HL]),
                     R=[('ue', ub)], W=[('carry', ch)])
                a.op('act', lambda e, ue=ue, ub=ub, cc=cc: e.activation(out=t1[ub][:, 0:N], in_=ue[:, 2:2 + N], func=AF.Identity,
                                                                  bias=cc[:, 3:4], scale=cc[:, 2:3]),
                     R=[('ue', ub), 'consts'], W=[('t1', ub)])
                a.op('dve', lambda e, ue=ue, ub=ub, cc=cc: e.scalar_tensor_tensor(out=t2[ub][:, 0:N], in0=ue[:, 1:1 + N], scalar=cc[:, 1:2],
                                                                            in1=t1[ub][:, 0:N], op0=ALU.mult, op1=ALU.add),
                     R=[('ue', ub), ('t1', ub), 'consts'], W=[('t2', ub)])
                a.op('dve', lambda e, ue=ue, ub=ub, cc=cc: e.scalar_tensor_tensor(out=t1[ub][:, 0:N], in0=ue[:, 0:N], scalar=cc[:, 0:1],
                                                                            in1=t2[ub][:, 0:N], op0=ALU.mult, op1=ALU.add),
                     R=[('ue', ub), ('t2', ub), 'consts'], W=[('t1', ub)])
                res.append(ub)
            if halo:
                continue
            g, u = res
            a.op('act', lambda e, g=g: e.activation(out=t2[g][:, 0:N], in_=t1[g][:, 0:N], func=AF.Silu),
                 R=[('t1', g)], W=[('t2', g)])
            a.op('dve', lambda e, g=g, u=u, j=j: e.tensor_tensor(out=big[:, j * NT:j * NT + N], in0=t2[g][:, 0:N], in1=t1[u][:, 0:N], op=ALU.mult),
                 R=[('t2', g), ('t1', u)], W=[('big', j)])
        if halo:
            continue
        for m in range(16):
            wb = cnt['wd'] % 2
            cnt['wd'] += 1
            a.dma('pool', wd_b[wb][:], w_dn[m].rearrange("p k j -> p (k j)"), W=[('wd', wb)], sk=('wd', wb))
            pb = m % 2
            for j in range(NFF):
                a.op('pe', lambda e, j=j, wb=wb, pb=pb: e.matmul(
                    ps_a[pb][:, 0:N], lhsT=wd_b[wb][:, j * 128:(j + 1) * 128], rhs=big[:, j * NT:j * NT + N],
                    start=(j == 0), stop=(j == NFF - 1)),
                    R=[('wd', wb), ('big', j)], W=[('psa', pb)], inc=(j == NFF - 1))
            if not final:
                o = cnt['ob'] % 2
                cnt['ob'] += 1
                a.op('dve', lambda e, m=m, pb=pb, o=o: e.tensor_tensor(out=ob[o][:, 0:N], in0=h1[:, m * NT:m * NT + N],
                                                                  in1=ps_a[pb][:, 0:N], op=ALU.add),
                     R=[('psa', pb), ('h1', m)], W=[('ob', o)])
                a.dma('sp', outT[m * 128:(m + 1) * 128, c0 - HL:c0 - HL + N], ob[o][:, 0:N], R=[('ob', o)], sk=('ob', o))
            else:
                a.op('dve', lambda e, m=m, pb=pb: e.tensor_tensor(out=h1[:, m * NT:m * NT + N], in0=h1[:, m * NT:m * NT + N],
                                                              in1=ps_a[pb][:, 0:N], op=ALU.add),
                     R=[('psa', pb), ('h1', m)], W=[('h1', m)])
        if final:
            a2 = kb.a
            for c in range(16):
                b = c % 2
                a2.op('act', lambda e, c=c, b=b: e.activation(out=sq[b][:, 0:N], in_=h1[:, c * NT:c * NT + N], func=AF.Square),
                      R=[('h1', c)], W=[('sq', b)])
                a2.op('pe', lambda e, c=c, b=b: e.matmul(ps_ssq[:, 0:N], lhsT=onesb[:, :], rhs=sq[b][:, 0:N],
                                                    start=(c == 0), stop=(c == 15)),
                      R=[('sq', b)], W=['ps_ssq'], inc=True)
            a2.op('act', lambda e: e.activation(out=rstd[:, 0:N], in_=ps_ssq[:, 0:N], func=AF.Sqrt, bias=epsb[:, :], scale=1.0 / DM),
                  R=['ps_ssq'], W=['rstd'])
            a2.op('dve', lambda e: e.reciprocal(out=rstd[:, 0:N], in_=rstd[:, 0:N]), R=['rstd'], W=['rstd'])
            for m in range(16):
                o = cnt['ob'] % 2
                cnt['ob'] += 1
                a2.op('dve', lambda e, m=m, o=o: e.scalar_tensor_tensor(out=ob[o][:, 0:N], in0=h1[:, m * NT:m * NT + N],
                                                                   scalar=cs[:, GN + m:GN + m + 1], in1=rstd[:, 0:N],
                                                                   op0=ALU.mult, op1=ALU.mult),
                      R=[('h1', m), 'rstd', 'consts'], W=[('ob', o)])
                a2.dma('sp', outT[m * 128:(m + 1) * 128, c0 - HL:c0 - HL + N], ob[o][:, 0:N], R=[('ob', o)], sk=('ob', o))
    return kb.close()


def tile_w(w, kchunks=None):
    K, M = w.shape
    return np.ascontiguousarray(w.reshape(K // 128, 128, M // 128, 128).transpose(2, 1, 0, 3))


def of_consts(ffn_gain, final_gain, conv_w, conv_b):
    c = np.zeros((128, 32 + 2 * NFF * 4), np.float32)
    c[:, 0:16] = ffn_gain.reshape(16, 128).T
    c[:, 16:32] = final_gain.reshape(16, 128).T
    cw = conv_w.reshape(3, 2 * NFF, 128)
    cb = conv_b.reshape(2 * NFF, 128)
    blk = np.concatenate([cw.transpose(2, 1, 0), cb.T[:, :, None]], axis=2)
    c[:, 32:] = blk.reshape(128, -1)
    return c


NH = 16
TWO_PI = 2.0 * np.pi


def rope_tables(kb, pos_i, N, tabc, posf, ang, cos2, sin2, nf, ni):
    a = kb.a
    C1 = 6.28125
    C2 = float(2.0 * np.pi - 6.28125)
    a.op('dve', lambda e: e.tensor_copy(out=posf[:, 0:N], in_=pos_i[:, 0:N]), R=['pos_i'], W=['posf'])
    for (col, dst, key) in ((1, cos2, 'cos2'), (2, sin2, 'sin2')):
        a.op('dve', lambda e, col=col: e.tensor_scalar(out=ang[:, 0:N], in0=posf[:, 0:N], scalar1=tabc[:, 0:1], scalar2=tabc[:, col:col + 1],
                                                      op0=ALU.mult, op1=ALU.add), R=['posf', 'consts'], W=['ang'])
        a.op('dve', lambda e: e.tensor_single_scalar(out=nf[:, 0:N], in_=ang[:, 0:N], scalar=float(1.0 / TWO_PI), op=ALU.mult), R=['ang'], W=['nf'])
        a.op('dve', lambda e: e.tensor_copy(out=ni[:, 0:N], in_=nf[:, 0:N]), R=['nf'], W=['ni'])
        a.op('dve', lambda e: e.tensor_copy(out=nf[:, 0:N], in_=ni[:, 0:N]), R=['ni'], W=['nf'])
        a.op('dve', lambda e: e.scalar_tensor_tensor(out=ang[:, 0:N], in0=nf[:, 0:N], scalar=-C1, in1=ang[:, 0:N], op0=ALU.mult, op1=ALU.add),
             R=['nf', 'ang'], W=['ang'])
        a.op('dve', lambda e: e.scalar_tensor_tensor(out=ang[:, 0:N], in0=nf[:, 0:N], scalar=-C2, in1=ang[:, 0:N], op0=ALU.mult, op1=ALU.add),
             R=['nf', 'ang'], W=['ang'])
        a.op('dve', lambda e: e.tensor_scalar(out=ang[:, 0:N], in0=ang[:, 0:N], scalar1=3.1415925, scalar2=-3.1415925, op0=ALU.min, op1=ALU.max),
             R=['ang'], W=['ang'])
        a.op('act', lambda e, dst=dst: e.activation(out=dst[:, 0:N], in_=ang[:, 0:N], func=AF.Sin), R=['ang'], W=[key])


def build_a(T):
    kb = KB()
    a = kb.a
    NT = 512
    xT = kb.din("xT", [DM, T], F32)
    pos = kb.din("pos", [64, T], I32)
    w_in = kb.din("w_in", [10, 128, 16, 128], F32)
    w_uq = kb.din("w_uq", [48, 128, 4, 128], F32)
    w_uk = kb.din("w_uk", [16, 128, 4, 128], F32)
    w_v = kb.din("w_v", [128, 4, 2048], F32)
    cst = kb.din("cst", [128, 16 + 4 + 4 + 4], F32)
    qnT = kb.dout("qnT", [NH, 128, T], BF16)
    qrT = kb.dout("qrT", [NH, 64, T], BF16)
    knT = kb.dout("knT", [NH, 128, T], BF16)
    krT = kb.dout("krT", [64, T], BF16)
    Vo = kb.dout("V", [T, 2048], BF16)

    xt = kb.sb("xt", [128, 16 * NT], F32)
    xn = kb.sb("xn", [128, 16 * NT], BF16)
    cq = kb.sb("cq", [128, 8 * NT], F32)
    cn = kb.sb("cn", [128, 8 * NT], BF16)
    sq = [kb.sb("sq%d" % i, [128, NT], BF16) for i in range(2)]
    rstd = kb.sb("rstd", [128, NT], F32)
    cs = kb.sb("cs", [128, 28], F32)
    onesb = kb.sb("onesb", [128, 128], BF16)
    epsb = kb.sb("epsb", [128, 1], F32)
    pos_i = kb.sb("pos_i", [64, NT], I32)
    posf = kb.sb("posf", [64, NT], F32)
    ang = kb.sb("ang", [64, NT], F32)
    nf = kb.sb("nf", [64, NT], F32)
    ni = kb.sb("ni", [64, NT], I32)
    cos2 = kb.sb("cos2", [64, NT], F32)
    sin2 = kb.sb("sin2", [64, NT], F32)
    ra = [kb.sb("ra%d" % i, [64, NT], F32) for i in range(2)]
    rb = [kb.sb("rb%d" % i, [64, NT], F32) for i in range(2)]
    ro = [kb.sb("ro%d" % i, [64, NT], BF16) for i in range(2)]
    ob = [kb.sb("ob%d" % i, [128, NT], BF16) for i in range(3)]
    win_b = [kb.sb("win%d" % i, [128, 16 * 128], BF16) for i in range(2)]
    wuq_s = kb.sb("wuq", [128, 48 * 512], BF16)
    wuk_s = kb.sb("wuk", [128, 16 * 512], BF16)
    wv_s = kb.sb("wv", [128, 4 * 2048], BF16)
    ps_a = [kb.ps("psa%d" % i, [128, NT]) for i in range(3)]
    ps_r = [kb.ps("psr%d" % i, [128, NT]) for i in range(4)]
    ps_ssq = kb.ps("ps_ssq", [128, NT])

    a.op('pool', lambda e: e.memset(onesb[:], 1.0), W=['consts'])
    a.op('pool', lambda e: e.memset(epsb[:], 1e-6), W=['consts'])
    a.dma('sp', cs[:], cst[:, :], W=['consts'], sk='cs')
    for i in range(6):
        a.dma('pool', wuq_s[:, i * 4096:(i + 1) * 4096].rearrange("p (c x) -> p c x", c=8),
              w_uq[i * 8:(i + 1) * 8].rearrange("c p k j -> p c (k j)"), W=[('wuq', i)], sk=('wuq', i))
    for i in range(2):
        a.dma('pool', wuk_s[:, i * 4096:(i + 1) * 4096].rearrange("p (c x) -> p c x", c=8),
              w_uk[i * 8:(i + 1) * 8].rearrange("c p k j -> p c (k j)"), W=[('wuk', i)], sk=('wuk', i))
    a.dma('pool', wv_s[:], w_v.rearrange("p k j -> p (k j)"), W=['wv'], sk='wv')
    GM, GQ, GK, RC = 0, 16, 20, 24
    cnt = dict(win=0, ob=0, r=0, pa=0)

    def emit_out(dst, ps_ap, rows, N):
        o = cnt['ob'] % 3
        cnt['ob'] += 1
        a.op('act', lambda e: e.copy(out=ob[o][0:rows, 0:N], in_=ps_ap), R=[pskey[0]], W=[('ob', o)])
        a.dma('sp', dst, ob[o][0:rows, 0:N], R=[('ob', o)], sk=('ob', o))

    pskey = [None]

    def rope_out(dst, pa_ap, pb_ap, ka, kbk, N):
        r = cnt['r'] % 2
        cnt['r'] += 1
        a.op('dve', lambda e: e.tensor_tensor(out=ra[r][:, 0:N], in0=pa_ap, in1=cos2[:, 0:N], op=ALU.mult),
             R=[ka, 'cos2'], W=[('ra', r)])
        a.op('dve', lambda e: e.tensor_tensor(out=rb[r][:, 0:N], in0=pb_ap, in1=sin2[:, 0:N], op=ALU.mult),
             R=[kbk, 'sin2'], W=[('rb', r)])
        a.op('dve', lambda e: e.tensor_tensor(out=ro[r][:, 0:N], in0=ra[r][:, 0:N], in1=rb[r][:, 0:N], op=ALU.add),
             R=[('ra', r), ('rb', r)], W=[('ro', r)])
        a.dma('sp', dst, ro[r][:, 0:N], R=[('ro', r)], sk=('ro', r))

    for ti in range(T // NT):
        c0 = ti * NT
        N = NT
        for m in range(16):
            a.dma('sp', xt[:, m * NT:m * NT + N], xT[m * 128:(m + 1) * 128, c0:c0 + N], W=[('xt', m)], sk=('xt', m))
        a.dma('sp', pos_i[:, 0:N], pos[:, c0:c0 + N], W=['pos_i'], sk='pos_i')
        rope_tables(kb, pos_i, N, cs[0:64, RC:RC + 4], posf, ang, cos2, sin2, nf, ni)
        rmsnorm_fm(kb, [xt[:, m * NT:m * NT + N] for m in range(16)], [('xt', m) for m in range(16)], 16, N,
                   cs[:, GM:GM + 16], onesb, epsb, ps_ssq, sq, rstd,
                   [xn[:, m * NT:m * NT + N] for m in range(16)], [('xn', m) for m in range(16)], DM)
        for m in range(10):
            wb = cnt['win'] % 2
            cnt['win'] += 1
            a.dma('pool', win_b[wb][:], w_in[m].rearrange("p k j -> p (k j)"), W=[('win', wb)], sk=('win', wb))
            M = 128 if m < 8 else 64
            if m < 8:
                pb = cnt['pa'] % 3
                cnt['pa'] += 1
                pst, pk = ps_a[pb], ('psa', pb)
            else:
                pb = m - 8
                pst, pk = ps_r[pb], ('psr', pb)
            for kc in range(16):
                a.op('pe', lambda e, kc=kc, wb=wb, pst=pst, M=M: e.matmul(
                    pst[0:M, 0:N], lhsT=win_b[wb][:, kc * 128:kc * 128 + M], rhs=xn[:, kc * NT:kc * NT + N],
                    start=(kc == 0), stop=(kc == 15)), R=[('win', wb), ('xn', kc)], W=[pk], inc=(kc == 15))
            if m < 8:
                a.op('act', lambda e, m=m, pst=pst: e.copy(out=cq[:, m * NT:m * NT + N], in_=pst[:, 0:N]), R=[pk], W=[('cq', m)])
        rope_out(krT[:, c0:c0 + N], ps_r[0][0:64, 0:N], ps_r[1][0:64, 0:N], ('psr', 0), ('psr', 1), N)
        for half, goff in ((0, GQ), (1, GK)):
            rmsnorm_fm(kb, [cq[:, (half * 4 + m) * NT:(half * 4 + m) * NT + N] for m in range(4)],
                       [('cq', half * 4 + m) for m in range(4)], 4, N, cs[:, goff:goff + 4], onesb, epsb, ps_ssq, sq, rstd,
                       [cn[:, (half * 4 + m) * NT:(half * 4 + m) * NT + N] for m in range(4)],
                       [('cn', half * 4 + m) for m in range(4)], 512)
        for h in range(NH):
            pb = cnt['pa'] % 3
            cnt['pa'] += 1
            for kc in range(4):
                a.op('pe', lambda e, kc=kc, h=h, pb=pb: e.matmul(
                    ps_a[pb][:, 0:N], lhsT=wuq_s[:, (3 * h) * 512 + kc * 128:(3 * h) * 512 + (kc + 1) * 128], rhs=cn[:, kc * NT:kc * NT + N],
                    start=(kc == 0), stop=(kc == 3)), R=[('wuq', (3 * h) // 8), ('cn', kc)], W=[('psa', pb)], inc=(kc == 3))
            pskey[0] = ('psa', pb)
            emit_out(qnT[h, :, c0:c0 + N], ps_a[pb][:, 0:N], 128, N)
            rp = []
            for v in range(2):
                ch = 3 * h + 1 + v
                pr = 2 + v if (h % 2) else v
                for kc in range(4):
                    a.op('pe', lambda e, kc=kc, ch=ch, pr=pr: e.matmul(
                        ps_r[pr][0:64, 0:N], lhsT=wuq_s[:, ch * 512 + kc * 128:ch * 512 + kc * 128 + 64], rhs=cn[:, kc * NT:kc * NT + N],
                        start=(kc == 0), stop=(kc == 3)), R=[('wuq', ch // 8), ('cn', kc)], W=[('psr', pr)], inc=(kc == 3))
                rp.append(pr)
            rope_out(qrT[h, :, c0:c0 + N], ps_r[rp[0]][0:64, 0:N], ps_r[rp[1]][0:64, 0:N], ('psr', rp[0]), ('psr', rp[1]), N)
            pb = cnt['pa'] % 3
            cnt['pa'] += 1
            for kc in range(4):
                a.op('pe', lambda e, kc=kc, h=h, pb=pb: e.matmul(
                    ps_a[pb][:, 0:N], lhsT=wuk_s[:, h * 512 + kc * 128:h * 512 + (kc + 1) * 128], rhs=cn[:, (4 + kc) * NT:(4 + kc) * NT + N],
                    start=(kc == 0), stop=(kc == 3)), R=[('wuk', h // 8), ('cn', 4 + kc)], W=[('psa', pb)], inc=(kc == 3))
            pskey[0] = ('psa', pb)
            emit_out(knT[h, :, c0:c0 + N], ps_a[pb][:, 0:N], 128, N)
        for tb in range(N // 128):
            for vg in range(4):
                pb = cnt['pa'] % 3
                cnt['pa'] += 1
                for kc in range(4):
                    a.op('pe', lambda e, kc=kc, tb=tb, vg=vg, pb=pb: e.matmul(
                        ps_a[pb][:, 0:512], lhsT=cn[:, (4 + kc) * NT + tb * 128:(4 + kc) * NT + (tb + 1) * 128],
                        rhs=wv_s[:, kc * 2048 + vg * 512:kc * 2048 + (vg + 1) * 512],
                        start=(kc == 0), stop=(kc == 3)), R=['wv', ('cn', 4 + kc)], W=[('psa', pb)], inc=(kc == 3))
                pskey[0] = ('psa', pb)
                emit_out(Vo[c0 + tb * 128:c0 + (tb + 1) * 128, vg * 512:(vg + 1) * 512], ps_a[pb][:, 0:512], 128, 512)
    return kb.close()


def a_weights(w_in, w_uq, w_ukv):
    z64 = np.zeros((DM, 64), np.float32)
    win_ext = np.concatenate([w_in[:, :1024], w_in[:, 1024:1088], z64,
                              w_in[:, 1056:1088], w_in[:, 1024:1056], z64], axis=1)
    q = w_uq.reshape(512, NH, 192)
    z = np.zeros((512, NH, 64), np.float32)
    q_ext = np.concatenate([q[:, :, :128], q[:, :, 128:192], z, q[:, :, 160:192], q[:, :, 128:160], z], axis=2)
    kv = w_ukv.reshape(512, NH, 256)
    wk = np.ascontiguousarray(kv[:, :, :128]).reshape(512, NH * 128)
    wv = np.ascontiguousarray(kv[:, :, 128:]).reshape(4, 128, NH * 128).transpose(1, 0, 2)
    return tile_w(win_ext), tile_w(q_ext.reshape(512, NH * 384)), tile_w(wk), np.ascontiguousarray(wv)


def a_consts(mla_gain, q_gain, kv_gain):
    c = np.zeros((128, 28), np.float32)
    c[:, 0:16] = mla_gain.reshape(16, 128).T
    c[:, 16:20] = q_gain.reshape(4, 128).T
    c[:, 20:24] = kv_gain.reshape(4, 128).T
    invf = (10000.0 ** (-np.arange(0, 64, 2, dtype=np.float32) / 64)).astype(np.float32)
    c[0:64, 24] = np.concatenate([invf, invf])
    c[0:64, 25] = np.pi / 2
    c[0:32, 26] = np.pi
    c[32:64, 26] = 0.0
    return c


def build_b(S, NHB):
    kb = KB()
    a = kb.a
    QB = 512
    qnT = kb.din("qnT", [NHB, 128, S], BF16)
    qrT = kb.din("qrT", [NHB, 64, S], BF16)
    knT = kb.din("knT", [NHB, 128, S], BF16)
    krT = kb.din("krT", [64, S], BF16)
    V = kb.din("V", [NHB, 128, S // 128, 128], BF16)
    msk = kb.din("msk", [128, 256], BF16)
    oT = kb.dout("oT", [NHB * 128, S], BF16)

    qn = [kb.sb("qn%d" % i, [128, S], BF16) for i in range(2)]
    qr = [kb.sb("qr%d" % i, [64, S], BF16) for i in range(2)]
    kn = [kb.sb("kn%d" % i, [128, S], BF16) for i in range(2)]
    vv = [kb.sb("vv%d" % i, [128, S], BF16) for i in range(2)]
    kr = kb.sb("kr", [64, S], BF16)
    mk = kb.sb("mk", [128, 256], BF16)
    onesb = kb.sb("onesb", [128, 128], BF16)
    pt = [kb.sb("pt%d" % i, [128, QB], BF16) for i in range(3)]
    rl = [kb.sb("rl%d" % i, [128, QB], F32) for i in range(2)]
    ot = [kb.sb("ot%d" % i, [128, QB], BF16) for i in range(2)]
    ps_s = [kb.ps("pss%d" % i, [128, QB]) for i in range(3)]
    ps_o = [kb.ps("pso%d" % i, [128, QB]) for i in range(2)]
    ps_l = [kb.ps("psl%d" % i, [128, QB]) for i in range(2)]
    scale = float(192 ** -0.5)

    a.op('pool', lambda e: e.memset(onesb[:], 1.0), W=['consts'])
    a.dma('sp', mk[:], msk[:, :], W=['consts'], sk='mk')
    a.dma('sp', kr[:], krT[:, :], W=['kr'], sk='kr')
    it = 0
    blk = 0
    for h in range(NHB):
        hb = h % 2
        a.dma('sp', qn[hb][:], qnT[h], W=[('qn', hb)], sk=('qn', hb))
        a.dma('sp', qr[hb][:], qrT[h], W=[('qr', hb)], sk=('qr', hb))
        a.dma('sp', kn[hb][:], knT[h], W=[('kn', hb)], sk=('kn', hb))
        a.dma('sp', vv[hb][:], V[h].rearrange("p k d -> p (k d)"), W=[('vv', hb)], sk=('vv', hb))
        for qb in range(S // QB):
            ob = blk % 2
            blk += 1
            nkb = 4 * (qb + 1)
            for kbi in range(nkb):
                j = kbi - 4 * qb
                c0 = 128 * j if j >= 0 else 0
                sbi = it % 3
                it += 1
                q0 = qb * QB
                a.op('pe', lambda e, sbi=sbi, kbi=kbi, c0=c0, q0=q0, hb=hb: e.matmul(
                    ps_s[sbi][:, c0:QB], lhsT=kn[hb][:, kbi * 128:(kbi + 1) * 128], rhs=qn[hb][:, q0 + c0:q0 + QB],
                    start=True, stop=False), R=[('kn', hb), ('qn', hb)], W=[('pss', sbi)], inc=False)
                a.op('pe', lambda e, sbi=sbi, kbi=kbi, c0=c0, q0=q0, hb=hb, j=j: e.matmul(
                    ps_s[sbi][:, c0:QB], lhsT=kr[:, kbi * 128:(kbi + 1) * 128], rhs=qr[hb][:, q0 + c0:q0 + QB],
                    start=False, stop=(j < 0)), R=['kr', ('qr', hb)], W=[('pss', sbi)], inc=(j < 0))
                if j >= 0:
                    a.op('pe', lambda e, sbi=sbi, c0=c0: e.matmul(
                        ps_s[sbi][:, c0:c0 + 128], lhsT=mk[:, 0:128], rhs=mk[:, 128:256],
                        start=False, stop=True, skip_group_check=True), R=['consts'], W=[('pss', sbi)], inc=True)
                a.op('act', lambda e, sbi=sbi, c0=c0: e.activation(out=pt[sbi][:, c0:QB], in_=ps_s[sbi][:, c0:QB], func=AF.Exp, scale=scale),
                     R=[('pss', sbi)], W=[('pt', sbi)])
                a.op('pe', lambda e, sbi=sbi, c0=c0, kbi=kbi, ob=ob, hb=hb, nkb=nkb: e.matmul(
                    ps_o[ob][:, c0:QB], lhsT=vv[hb][:, kbi * 128:(kbi + 1) * 128], rhs=pt[sbi][:, c0:QB],
                    start=(kbi == 0), stop=(kbi == nkb - 1), skip_group_check=True), R=[('vv', hb), ('pt', sbi)], W=[('pso', ob)], inc=False)
                a.op('pe', lambda e, sbi=sbi, c0=c0, kbi=kbi, ob=ob, nkb=nkb: e.matmul(
                    ps_l[ob][:, c0:QB], lhsT=onesb[:, :], rhs=pt[sbi][:, c0:QB],
                    start=(kbi == 0), stop=(kbi == nkb - 1), skip_group_check=True), R=['consts', ('pt', sbi)], W=[('psl', ob)], inc=True)
            a.op('dve', lambda e, ob=ob: e.reciprocal(out=rl[ob][:, :], in_=ps_l[ob][:, :]), R=[('psl', ob)], W=[('rl', ob)])
            a.op('dve', lambda e, ob=ob: e.tensor_tensor(out=ot[ob][:, :], in0=ps_o[ob][:, :], in1=rl[ob][:, :], op=ALU.mult),
                 R=[('pso', ob), ('rl', ob)], W=[('ot', ob)])
            a.dma('sp', oT[h * 128:(h + 1) * 128, qb * QB:(qb + 1) * QB], ot[ob][:, :], R=[('ot', ob)], sk=('ot', ob))
    return kb.close()


def b_mask():
    m = np.zeros((128, 256), np.float32)
    m[:, 0:128] = np.eye(128)
    k = np.arange(128)[:, None]
    q = np.arange(128)[None, :]
    m[:, 128:256] = np.where(q >= k, 0.0, -30000.0)
    return m.astype(NPBF)


HD = 3
NCV = 64


def build_d(T):
    kb = KB()
    a = kb.a
    NT = 512
    hT = kb.din("hT", [DM, HD + T], F32)
    w_in = kb.din("w_in", [97, 128, 16, 128], F32)
    cst = kb.din("cst", [128, 16 + NCV * 4 + 4], F32)
    qT = kb.dout("qT", [16, 128, T], BF16)
    kT = kb.dout("kT", [16, 128, T], BF16)
    vT = kb.dout("vT", [32, 128, T], BF16)
    szT = kb.dout("szT", [32, 128, T], BF16)
    gbT = kb.dout("gbT", [64, T], F32)

    ht = kb.sb("ht", [128, 16 * NT], F32)
    xn = kb.sb("xn", [128, 16 * NT], BF16)
    sq = [kb.sb("sq%d" % i, [128, NT], BF16) for i in range(2)]
    rstd = kb.sb("rstd", [128, NT], F32)
    cs = kb.sb("cs", [128, 16 + NCV * 4 + 4], F32)
    coef = kb.sb("coef", [64, 1], F32)
    onesb = kb.sb("onesb", [128, 128], BF16)
    epsb = kb.sb("epsb", [128, 1], F32)
    carry = kb.sb("carry", [128, NCV * HD], F32)
    uext = [kb.sb("uext%d" % i, [128, HD + NT], F32) for i in range(2)]
    t1 = [kb.sb("t1_%d" % i, [128, NT], F32) for i in range(2)]
    t2 = [kb.sb("t2_%d" % i, [128, NT], F32) for i in range(2)]
    rn = [kb.sb("rn%d" % i, [128, NT], F32) for i in range(2)]
    s2 = [kb.sb("s2_%d" % i, [128, NT], BF16) for i in range(2)]
    ob = [kb.sb("ob%d" % i, [128, NT], BF16) for i in range(3)]
    gx = kb.sb("gx", [64, NT], F32)
    gy = kb.sb("gy", [64, NT], F32)
    go = [kb.sb("go%d" % i, [64, NT], F32) for i in range(2)]
    wu_b = [kb.sb("wu%d" % i, [128, 16 * 128], BF16) for i in range(4)]
    ps_u = [kb.ps("psu%d" % i, [128, NT]) for i in range(4)]
    ps_ssq = kb.ps("ps_ssq", [128, NT])
    ps_n = [kb.ps("psn%d" % i, [128, NT]) for i in range(2)]

    a.op('pool', lambda e: e.memset(onesb[:], 1.0), W=['consts'])
    a.op('pool', lambda e: e.memset(epsb[:], 1e-6), W=['consts'])
    a.dma('sp', cs[:], cst[:, :], W=['consts'], sk='cs')
    GA = 16 + NCV * 4
    a.op('act', lambda e: e.activation(out=coef[0:64, :], in_=cs[0:64, GA + 2:GA + 3], func=AF.Exp), R=['consts'], W=['coef'])
    a.op('dve', lambda e: e.tensor_single_scalar(out=coef[0:64, :], in_=coef[0:64, :], scalar=-1.0, op=ALU.mult), R=['coef'], W=['coef'])
    cnt = dict(wu=0, ue=0, ob=0, n=0, go=0)
    tiles = [(0, HD, True)] + [(HD + i * NT, NT, False) for i in range(T // NT)]
    for (c0, N, halo) in tiles:
        for m in range(16):
            a.dma('sp', ht[:, m * NT:m * NT + N], hT[m * 128:(m + 1) * 128, c0:c0 + N], W=[('ht', m)], sk=('ht', m))
        rmsnorm_fm(kb, [ht[:, m * NT:m * NT + N] for m in range(16)], [('ht', m) for m in range(16)], 16, N,
                   cs[:, 0:16], onesb, epsb, ps_ssq, sq, rstd,
                   [xn[:, m * NT:m * NT + N] for m in range(16)], [('xn', m) for m in range(16)], DM)
        for ch in range(NCV if halo else 97):
            wb = cnt['wu'] % 4
            cnt['wu'] += 1
            a.dma('pool', wu_b[wb][:], w_in[ch].rearrange("p k j -> p (k j)"), W=[('wu', wb)], sk=('wu', wb))
            pb = wb
            M = 128 if ch < 96 else 64
            for kc in range(16):
                a.op('pe', lambda e, kc=kc, wb=wb, pb=pb, M=M: e.matmul(
                    ps_u[pb][0:M, 0:N], lhsT=wu_b[wb][:, kc * 128:kc * 128 + M], rhs=xn[:, kc * NT:kc * NT + N],
                    start=(kc == 0), stop=(kc == 15)), R=[('wu', wb), ('xn', kc)], W=[('psu', pb)], inc=(kc == 15))
            c1 = c0 - HD
            if ch < NCV:
                cr = carry[:, ch * HD:(ch + 1) * HD]
                if halo:
                    a.op('act', lambda e, cr=cr, pb=pb: e.copy(out=cr, in_=ps_u[pb][:, 0:HD]), R=[('psu', pb)], W=[('carry', ch)])
                    continue
                ub = cnt['ue'] % 2
                cnt['ue'] += 1
                ue = uext[ub]
                cc = cs[:, 16 + ch * 4:16 + ch * 4 + 4]
                a.op('act', lambda e, ue=ue, pb=pb: e.copy(out=ue[:, HD:HD + N], in_=ps_u[pb][:, 0:N]), R=[('psu', pb)], W=[('ue', ub)])
                a.op('act', lambda e, ue=ue, cr=cr: e.copy(out=ue[:, 0:HD], in_=cr), R=[('carry', ch)], W=[('ue', ub)])
                a.op('act', lambda e, ue=ue, cr=cr: e.copy(out=cr, in_=ue[:, N:N + HD]), R=[('ue', ub)], W=[('carry', ch)])
                a.op('dve', lambda e, ue=ue, ub=ub, cc=cc: e.tensor_single_scalar(out=t1[ub][:, 0:N], in_=ue[:, 3:3 + N], scalar=cc[:, 3:4], op=ALU.mult),
                     R=[('ue', ub), 'consts'], W=[('t1', ub)])
                a.op('dve', lambda e, ue=ue, ub=ub, cc=cc: e.scalar_tensor_tensor(out=t2[ub][:, 0:N], in0=ue[:, 2:2 + N], scalar=cc[:, 2:3],
                                                                            in1=t1[ub][:, 0:N], op0=ALU.mult, op1=ALU.add),
                     R=[('ue', ub), ('t1', ub), 'consts'], W=[('t2', ub)])
                a.op('dve', lambda e, ue=ue, ub=ub, cc=cc: e.scalar_tensor_tensor(out=t1[ub][:, 0:N], in0=ue[:, 1:1 + N], scalar=cc[:, 1:2],
                                                                            in1=t2[ub][:, 0:N], op0=ALU.mult, op1=ALU.add),
                     R=[('ue', ub), ('t2', ub), 'consts'], W=[('t1', ub)])
                a.op('dve', lambda e, ue=ue, ub=ub, cc=cc: e.scalar_tensor_tensor(out=t2[ub][:, 0:N], in0=ue[:, 0:N], scalar=cc[:, 0:1],
                                                                            in1=t1[ub][:, 0:N], op0=ALU.mult, op1=ALU.add),
                     R=[('ue', ub), ('t1', ub), 'consts'], W=[('t2', ub)])
                o = cnt['ob'] % 3
                cnt['ob'] += 1
                if ch >= 32:
                    a.op('act', lambda e, ub=ub, o=o: e.activation(out=ob[o][:, 0:N], in_=t2[ub][:, 0:N], func=AF.Silu), R=[('t2', ub)], W=[('ob', o)])
                    a.dma('sp', vT[ch - 32, :, c1:c1 + N], ob[o][:, 0:N], R=[('ob', o)], sk=('ob', o))
                    continue
                a.op('act', lambda e, ub=ub: e.activation(out=t1[ub][:, 0:N], in_=t2[ub][:, 0:N], func=AF.Silu), R=[('t2', ub)], W=[('t1', ub)])
                nb = cnt['n'] % 2
                cnt['n'] += 1
                a.op('act', lambda e, ub=ub, nb=nb: e.activation(out=s2[nb][:, 0:N], in_=t1[ub][:, 0:N], func=AF.Square), R=[('t1', ub)], W=[('s2', nb)])
                a.op('pe', lambda e, nb=nb: e.matmul(ps_n[nb][:, 0:N], lhsT=onesb[:, :], rhs=s2[nb][:, 0:N], start=True, stop=True),
                     R=[('s2', nb), 'consts'], W=[('psn', nb)])
                a.op('act', lambda e, nb=nb: e.activation(out=rn[nb][:, 0:N], in_=ps_n[nb][:, 0:N], func=AF.Sqrt, bias=epsb[:, :], scale=1.0),
                     R=[('psn', nb), 'consts'], W=[('rn', nb)])
                a.op('dve', lambda e, nb=nb: e.reciprocal(out=rn[nb][:, 0:N], in_=rn[nb][:, 0:N]), R=[('rn', nb)], W=[('rn', nb)])
                sc = float(128 ** -0.5) if ch < 16 else 1.0
                a.op('dve', lambda e, ub=ub, nb=nb, o=o, sc=sc: e.scalar_tensor_tensor(out=ob[o][:, 0:N], in0=t1[ub][:, 0:N], scalar=sc,
                                                                                 in1=rn[nb][:, 0:N], op0=ALU.mult, op1=ALU.mult),
                     R=[('t1', ub), ('rn', nb)], W=[('ob', o)])
                dst = qT[ch, :, c1:c1 + N] if ch < 16 else kT[ch - 16, :, c1:c1 + N]
                a.dma('sp', dst, ob[o][:, 0:N], R=[('ob', o)], sk=('ob', o))
            elif ch < 96:
                o = cnt['ob'] % 3
                cnt['ob'] += 1
                a.op('act', lambda e, pb=pb, o=o: e.activation(out=ob[o][:, 0:N], in_=ps_u[pb][:, 0:N], func=AF.Silu), R=[('psu', pb)], W=[('ob', o)])
                a.dma('sp', szT[ch - 64, :, c1:c1 + N], ob[o][:, 0:N], R=[('ob', o)], sk=('ob', o))
            else:
                a.op('act', lambda e, pb=pb: e.activation(out=gx[:, 0:N], in_=ps_u[pb][0:64, 0:N], func=AF.Identity,
                                                       bias=cs[0:64, GA + 1:GA + 2], scale=cs[0:64, GA:GA + 1]), R=[('psu', pb), 'consts'], W=['gx'])
                a.op('act', lambda e: e.activation(out=gy[:, 0:N], in_=gx[:, 0:N], func=AF.Abs), R=['gx'], W=['gy'])
                a.op('act', lambda e: e.activation(out=gy[:, 0:N], in_=gy[:, 0:N], func=AF.Exp, scale=-1.0), R=['gy'], W=['gy'])
                a.op('act', lambda e: e.activation(out=gy[:, 0:N], in_=gy[:, 0:N], func=AF.Ln, bias=epsb_one(kb)[0:64, :], scale=1.0), R=['gy', 'consts'], W=['gy'])
                a.op('dve', lambda e: e.scalar_tensor_tensor(out=gx[:, 0:N], in0=gx[:, 0:N], scalar=0.0, in1=gy[:, 0:N], op0=ALU.max, op1=ALU.add),
                     R=['gx', 'gy'], W=['gx'])
                g = cnt['go'] % 2
                cnt['go'] += 1
                a.op('dve', lambda e, g=g: e.tensor_single_scalar(out=go[g][:, 0:N], in_=gx[:, 0:N], scalar=coef[0:64, 0:1], op=ALU.mult),
                     R=['gx', 'coef'], W=[('go', g)])
                a.dma('sp', gbT[:, c1:c1 + N], go[g][:, 0:N], R=[('go', g)], sk=('go', g))
    return kb.close()


def epsb_one(kb):
    if not hasattr(kb, '_one'):
        kb._one = kb.sb("one_f", [128, 1], F32)
        kb.a.op('pool', lambda e: e.memset(kb._one[:], 1.0), W=['consts'])
    return kb._one


def d_weights(w_in):
    pad = np.zeros((DM, 64), np.float32)
    return tile_w(np.concatenate([w_in, pad], axis=1))


def d_consts(gain, conv_w, a_log, dt_bias):
    c = np.zeros((128, 16 + NCV * 4 + 4), np.float32)
    c[:, 0:16] = gain.reshape(16, 128).T
    cw = conv_w.reshape(4, NCV, 128)
    c[:, 16:16 + NCV * 4] = cw.transpose(2, 1, 0).reshape(128, -1)
    GA = 16 + NCV * 4
    c[0:32, GA] = -1.0
    c[32:64, GA] = 1.0
    c[32:64, GA + 1] = dt_bias
    c[0:32, GA + 2] = 0.0
    c[32:64, GA + 2] = a_log
    return c


CH = 128
NEGB = -30000.0


def build_e(S, NG):
    kb = KB()
    a = kb.a
    nc = kb.nc
    NCHK = S // CH
    NV = 4 * NG
    NQ = 2 * NG
    qTc = kb.din("qTc", [NCHK, 128, NQ * 128], BF16)
    kTc = kb.din("kTc", [NCHK, 128, NQ * 128], BF16)
    ktm = kb.din("ktm", [NCHK, 128, NQ * 128], BF16)
    vtm = kb.din("vtm", [NCHK, 128, NV * 128], BF16)
    szc = kb.din("szc", [NCHK, 128, NV * 128], BF16)
    glb = kb.din("glb", [NCHK, 128, 2 * NV], F32)
    cf = kb.din("cf", [128, 3 * 128 + 1], F32)
    cb16 = kb.din("cb16", [128, 4 * 128], BF16)
    selc = kb.din("selc", [16, 2 * 16 * 128], F32)
    oT = kb.dout("oT", [NV * 128, S], BF16)

    NB = 3
    qc = [kb.sb("qc%d" % i, [128, NQ * 128], BF16) for i in range(NB)]
    kc = [kb.sb("kc%d" % i, [128, NQ * 128], BF16) for i in range(NB)]
    kt = [kb.sb("kt%d" % i, [128, NQ * 128], BF16) for i in range(NB)]
    vt = [kb.sb("vt%d" % i, [128, NV * 128], BF16) for i in range(NB)]
    sz = [kb.sb("sz%d" % i, [128, NV * 128], BF16) for i in range(NB)]
    gl = [kb.sb("gl%d" % i, [128, 2 * NV], F32) for i in range(NB)]
    stt = [kb.sb("stt%d" % i, [128, 7 * NV], F32) for i in range(NB)]
    gcT = [kb.sb("gcT%d" % i, [16, 128], F32) for i in range(NB)]
    avT = [kb.sb("avT%d" % i, [16, 128], F32) for i in range(NB)]
    cfs = kb.sb("cfs", [128, 3 * 128 + 1], F32)
    cbs = kb.sb("cbs", [128, 4 * 128], BF16)
    sels = kb.sb("sels", [16, 2 * 16 * 128], F32)
    onesb = kb.sb("onesb", [128, 128], BF16)
    epsb = kb.sb("epsb", [128, 1], F32)
    Sf = kb.sb("Sf", [128, NV * 128], F32)
    Sb = kb.sb("Sb", [128, NV * 128], BF16)
    NSL = 3
    names_f = ["E", "NT", "Nm", "Pa", "Pb", "PTa", "PTb", "Y"]
    names_b = ["Yu", "Yw", "WTn", "AT", "QgT", "Kd", "Vnb"]
    U = {}
    for sl in range(NSL):
        for u in range(4):
            for nm in names_b:
                U[(nm, sl, u)] = kb.sb("%s_%d_%d" % (nm, sl, u), [128, 128], BF16)
            for nm in names_f:
                U[(nm, sl, u)] = kb.sb("%s_%d_%d" % (nm, sl, u), [128, 128], F32)
    sqo = kb.sb("sqo", [128, 512], BF16)
    rso = kb.sb("rso", [128, 512], F32)
    to = kb.sb("to", [128, 512], F32)
    oo = [kb.sb("oo%d" % i, [128, 512], BF16) for i in range(2)]
    PS = [kb.ps("psb%d" % i, [128, 512]) for i in range(8)]

    def pt(b, t):
        return PS[b][:, t * 128:(t + 1) * 128], ('ps', b)

    TRI, SELL, IDF, GAIN = 0, 128, 256, 384
    IDB, MSL, MSU, MU = 0, 128, 256, 384
    a.op('pool', lambda e: e.memset(onesb[:], 1.0), W=['consts'])
    a.op('pool', lambda e: e.memset(epsb[:], 1e-6), W=['consts'])
    a.op('pool', lambda e: e.memset(Sf[:], 0.0), W=[('Sf', h) for h in range(NV)])
    a.op('pool', lambda e: e.memset(Sb[:], 0.0), W=[('Sb', h) for h in range(NV)])
    a.dma('sp', cfs[:], cf[:, :], W=['consts'], sk='cfs')
    a.dma('sp', cbs[:], cb16[:, :], W=['consts'], sk='cbs')
    a.dma('sp', sels[:], selc[:, :], W=['consts'], sk='sels')
    GC, AV, NGC, DL, EGL, EA, BT = [i * NV for i in range(7)]

    def chunk_prep(n):
        cb = n % NB
        for (dst, src, nm) in ((qc, qTc, 'qc'), (kc, kTc, 'kc'), (kt, ktm, 'kt'), (vt, vtm, 'vt'), (sz, szc, 'sz'), (gl, glb, 'gl')):
            a.dma('sp', dst[cb][:], src[n], W=[(nm, cb)], sk=(nm, cb))
        st = stt[cb]
        p0, k0 = pt(5, 0)
        p1, k1 = pt(5, 1)
        p2, k2 = pt(5, 2)
        p3, k3 = pt(5, 3)
        a.op('pe', lambda e: e.matmul(p0[:, 0:NV], lhsT=cfs[:, TRI:TRI + 128], rhs=gl[cb][:, 0:NV], start=True, stop=True),
             R=[('gl', cb), 'consts'], W=[k0])
        a.op('act', lambda e: e.copy(out=st[:, GC:GC + NV], in_=p0[:, 0:NV]), R=[k0], W=[('stt', cb)])
        a.op('dve', lambda e: e.tensor_tensor(out=st[:, AV:AV + NV], in0=st[:, GC:GC + NV], in1=gl[cb][:, NV:2 * NV], op=ALU.add),
             R=[('stt', cb), ('gl', cb)], W=[('stt', cb)])
        a.op('dve', lambda e: e.tensor_single_scalar(out=st[:, NGC:NGC + NV], in_=st[:, GC:GC + NV], scalar=-1.0, op=ALU.mult),
             R=[('stt', cb)], W=[('stt', cb)])
        a.op('pe', lambda e: e.matmul(p1[:, 0:NV], lhsT=cfs[:, SELL:SELL + 128], rhs=st[:, GC:GC + NV], start=True, stop=True),
             R=[('stt', cb), 'consts'], W=[k1])
        a.op('dve', lambda e: e.tensor_tensor(out=st[:, DL:DL + NV], in0=p1[:, 0:NV], in1=st[:, GC:GC + NV], op=ALU.subtract),
             R=[k1, ('stt', cb)], W=[('stt', cb)])
        a.op('act', lambda e: e.activation(out=st[:, DL:DL + NV], in_=st[:, DL:DL + NV], func=AF.Exp), R=[('stt', cb)], W=[('stt', cb)])
        a.op('act', lambda e: e.activation(out=st[:, EGL:EGL + NV], in_=p1[:, 0:NV], func=AF.Exp), R=[k1], W=[('stt', cb)])
        a.op('act', lambda e: e.activation(out=st[:, EA:EA + NV], in_=st[:, AV:AV + NV], func=AF.Exp), R=[('stt', cb)], W=[('stt', cb)])
        a.op('act', lambda e: e.activation(out=st[:, BT:BT + NV], in_=gl[cb][:, NV:2 * NV], func=AF.Exp), R=[('gl', cb)], W=[('stt', cb)])
        a.op('pe', lambda e: e.transpose(p2[0:NV, :], st[:, GC:GC + NV], cfs[:, IDF:IDF + 128]), R=[('stt', cb), 'consts'], W=[k2])
        a.op('act', lambda e: e.copy(out=gcT[cb][0:NV, :], in_=p2[0:NV, :]), R=[k2], W=[('gcT', cb)])
        a.op('pe', lambda e: e.transpose(p3[0:NV, :], st[:, AV:AV + NV], cfs[:, IDF:IDF + 128]), R=[('stt', cb), 'consts'], W=[k3])
        a.op('act', lambda e: e.copy(out=avT[cb][0:NV, :], in_=p3[0:NV, :]), R=[k3], W=[('avT', cb)])

    def pre(n, gi, sl):
        cb = n % NB
        st = stt[cb]
        Ub = lambda nm, u: U[(nm, sl, u)]
        Uk = lambda nm, u: (nm, sl, u)
        for qh in range(2):
            hq = 2 * gi + qh
            ks = kc[cb][:, hq * 128:(hq + 1) * 128]
            g_ap, g_k = pt(0, qh)
            a.op('pe', lambda e, ks=ks, g_ap=g_ap: e.matmul(g_ap, lhsT=ks, rhs=ks, start=True, stop=True), R=[('kc', cb)], W=[g_k])
            q_ap, q_k = pt(0, 2 + qh)
            a.op('pe', lambda e, ks=ks, q_ap=q_ap, hq=hq: e.matmul(q_ap, lhsT=ks, rhs=qc[cb][:, hq * 128:(hq + 1) * 128], start=True, stop=True),
                 R=[('kc', cb), ('qc', cb)], W=[q_k])
        stages = [(16, 'gcT', MSL, AV, 0, -1.0, 'NT'), (0, 'avT', MSU, NGC, 0, -1.0, 'Nm'),
                  (0, 'gcT', MU, NGC, 2, 1.0, 'AT'), (0, 'gcT', None, None, None, None, 'QgT')]
        for si, (soff, tnm, mcol, bcol, src_t, sgn, dst) in enumerate(stages):
            bank = 1 if si % 2 == 0 else 2
            tsrc = gcT[cb] if tnm == 'gcT' else avT[cb]
            for u in range(4):
                hv = 4 * gi + u
                r_ap, r_k = pt(bank, u)
                a.op('pe', lambda e, r_ap=r_ap, soff=soff, hv=hv, tsrc=tsrc, mcol=mcol: e.matmul(
                    r_ap, lhsT=sels[0:NV, (soff + hv) * 128:(soff + hv + 1) * 128], rhs=tsrc[0:NV, :], start=True, stop=(mcol is None)),
                    R=[(tnm, cb), 'consts'], W=[r_k], inc=(mcol is None))
                if mcol is not None:
                    a.op('pe', lambda e, r_ap=r_ap, mcol=mcol: e.matmul(r_ap, lhsT=cbs[:, IDB:IDB + 128], rhs=cbs[:, mcol:mcol + 128],
                                                                   start=False, stop=True), R=['consts'], W=[r_k])
            for u in range(4):
                hv = 4 * gi + u
                r_ap, r_k = pt(bank, u)
                E = Ub("E", u)
                if bcol is not None:
                    a.op('act', lambda e, E=E, r_ap=r_ap, bcol=bcol, hv=hv: e.activation(out=E[:, :], in_=r_ap, func=AF.Exp, bias=st[:, bcol + hv:bcol + hv + 1], scale=1.0),
                         R=[r_k, ('stt', cb)], W=[Uk("E", u)])
                    s_ap, s_k = pt(0, src_t + u // 2)
                    a.op('dve', lambda e, E=E, s_ap=s_ap, sgn=sgn, dst=dst, u=u: e.scalar_tensor_tensor(
                        out=Ub(dst, u)[:, :], in0=s_ap, scalar=sgn, in1=E[:, :], op0=ALU.mult, op1=ALU.mult),
                        R=[s_k, Uk("E", u)], W=[Uk(dst, u)])
                else:
                    a.op('act', lambda e, E=E, r_ap=r_ap: e.activation(out=E[:, :], in_=r_ap, func=AF.Exp), R=[r_k], W=[Uk("E", u)])
                    hq = 2 * gi + u // 2
                    a.op('pool', lambda e, E=E, u=u, hq=hq: e.tensor_tensor(out=Ub("QgT", u)[:, :], in0=qc[cb][:, hq * 128:(hq + 1) * 128], in1=E[:, :], op=ALU.mult),
                         R=[('qc', cb), Uk("E", u)], W=[Uk("QgT", u)])
        for u in range(4):
            hv = 4 * gi + u
            hq = 2 * gi + u // 2
            a.op('pool', lambda e, u=u: e.tensor_tensor(out=Ub("Y", u)[:, :], in0=Ub("Nm", u)[:, :], in1=cfs[:, IDF:IDF + 128], op=ALU.add),
                 R=[Uk("Nm", u), 'consts'], W=[Uk("Y", u)])
            a.op('pool', lambda e, u=u, hq=hq, hv=hv: e.tensor_single_scalar(out=Ub("Kd", u)[:, :], in_=kt[cb][:, hq * 128:(hq + 1) * 128],
                                                                          scalar=st[:, DL + hv:DL + hv + 1], op=ALU.mult),
                 R=[('kt', cb), ('stt', cb)], W=[Uk("Kd", u)])
        P = {u: "Nm" for u in range(4)}
        PT = {u: "NT" for u in range(4)}
        for lv in range(6):
            nP = "Pa" if lv % 2 == 0 else "Pb"
            nPT = "PTa" if lv % 2 == 0 else "PTb"
            for u in range(4):
                t_ap, t_k = pt(4, u)
                a.op('pe', lambda e, u=u, t_ap=t_ap: e.matmul(t_ap, lhsT=Ub(P[u], u)[:, :], rhs=Ub(PT[u], u)[:, :], start=True, stop=True),
                     R=[Uk(P[u], u), Uk(PT[u], u)], W=[t_k])
                if lv < 5:
                    p_ap, p_k = pt(5, u)
                    a.op('pe', lambda e, u=u, p_ap=p_ap: e.matmul(p_ap, lhsT=Ub(PT[u], u)[:, :], rhs=Ub(P[u], u)[:, :], start=True, stop=True),
                         R=[Uk(P[u], u), Uk(PT[u], u)], W=[p_k])
            for u in range(4):
                t_ap, t_k = pt(4, u)
                a.op('act', lambda e, u=u, t_ap=t_ap: e.copy(out=Ub(nPT, u)[:, :], in_=t_ap), R=[t_k], W=[Uk(nPT, u)])
                if lv < 5:
                    p_ap, p_k = pt(5, u)
                    a.op('dve', lambda e, u=u, p_ap=p_ap: e.tensor_copy(out=Ub(nP, u)[:, :], in_=p_ap), R=[p_k], W=[Uk(nP, u)])
            for u in range(4):
                y_ap, y_k = pt(6, u)
                a.op('pe', lambda e, u=u, y_ap=y_ap: e.matmul(y_ap, lhsT=Ub(nPT, u)[:, :], rhs=Ub("Y", u)[:, :], start=True, stop=True),
                     R=[Uk(nPT, u), Uk("Y", u)], W=[y_k])
            for u in range(4):
                y_ap, y_k = pt(6, u)
                a.op('dve', lambda e, u=u, y_ap=y_ap: e.tensor_tensor(out=Ub("Y", u)[:, :], in0=Ub("Y", u)[:, :], in1=y_ap, op=ALU.add),
                     R=[y_k, Uk("Y", u)], W=[Uk("Y", u)])
            for u in range(4):
                P[u] = nP
                PT[u] = nPT
        for u in range(4):
            hv = 4 * gi + u
            hq = 2 * gi + u // 2
            a.op('act', lambda e, u=u, hv=hv: e.activation(out=Ub("Yu", u)[:, :], in_=Ub("Y", u)[:, :], func=AF.Copy, scale=st[:, BT + hv:BT + hv + 1]),
                 R=[Uk("Y", u), ('stt', cb)], W=[Uk("Yu", u)])
            a.op('act', lambda e, u=u, hv=hv: e.activation(out=Ub("Yw", u)[:, :], in_=Ub("Y", u)[:, :], func=AF.Copy, scale=st[:, EA + hv:EA + hv + 1]),
                 R=[Uk("Y", u), ('stt', cb)], W=[Uk("Yw", u)])
            w_ap, w_k = pt(1, u)
            a.op('pe', lambda e, u=u, hq=hq, w_ap=w_ap: e.matmul(w_ap, lhsT=kt[cb][:, hq * 128:(hq + 1) * 128], rhs=Ub("Yw", u)[:, :], start=True, stop=True),
                 R=[('kt', cb), Uk("Yw", u)], W=[w_k])
            a.op('act', lambda e, u=u, w_ap=w_ap: e.activation(out=Ub("WTn", u)[:, :], in_=w_ap, func=AF.Copy, scale=-1.0), R=[w_k], W=[Uk("WTn", u)])

    ocnt = [0]

    def rec(n, gi, sl):
        cb = n % NB
        st = stt[cb]
        Ub = lambda nm, u: U[(nm, sl, u)]
        Uk = lambda nm, u: (nm, sl, u)
        for u in range(4):
            hv = 4 * gi + u
            v_ap, v_k = pt(7, u)
            a.op('pe', lambda e, u=u, hv=hv, v_ap=v_ap: e.matmul(v_ap, lhsT=Ub("Yu", u)[:, :], rhs=vt[cb][:, hv * 128:(hv + 1) * 128], start=True, stop=False),
                 R=[Uk("Yu", u), ('vt', cb)], W=[v_k], inc=False)
            a.op('pe', lambda e, u=u, hv=hv, v_ap=v_ap: e.matmul(v_ap, lhsT=Ub("WTn", u)[:, :], rhs=Sb[:, hv * 128:(hv + 1) * 128], start=False, stop=True),
                 R=[Uk("WTn", u), ('Sb', hv)], W=[v_k])
        for u in range(4):
            v_ap, v_k = pt(7, u)
            a.op('act', lambda e, u=u, v_ap=v_ap: e.copy(out=Ub("Vnb", u)[:, :], in_=v_ap), R=[v_k], W=[Uk("Vnb", u)])
        for u in range(4):
            hv = 4 * gi + u
            o_ap, o_k = pt(3, u)
            a.op('pe', lambda e, u=u, hv=hv, o_ap=o_ap: e.matmul(o_ap, lhsT=Sb[:, hv * 128:(hv + 1) * 128], rhs=Ub("QgT", u)[:, :], start=True, stop=False),
                 R=[('Sb', hv), Uk("QgT", u)], W=[o_k], inc=False)
            a.op('pe', lambda e, u=u, o_ap=o_ap: e.matmul(o_ap, lhsT=Ub("Vnb", u)[:, :], rhs=Ub("AT", u)[:, :], start=False, stop=True),
                 R=[Uk("Vnb", u), Uk("AT", u)], W=[o_k])
            s_ap, s_k = pt(6, u)
            a.op('pe', lambda e, u=u, s_ap=s_ap: e.matmul(s_ap, lhsT=Ub("Kd", u)[:, :], rhs=Ub("Vnb", u)[:, :], start=True, stop=True),
                 R=[Uk("Kd", u), Uk("Vnb", u)], W=[s_k])
        for u in range(4):
            hv = 4 * gi + u
            s_ap, s_k = pt(6, u)
            a.op('dve', lambda e, hv=hv, s_ap=s_ap: e.scalar_tensor_tensor(out=Sf[:, hv * 128:(hv + 1) * 128], in0=Sf[:, hv * 128:(hv + 1) * 128],
                                                                      scalar=st[:, EGL + hv:EGL + hv + 1], in1=s_ap, op0=ALU.mult, op1=ALU.add),
                 R=[s_k, ('Sf', hv), ('stt', cb)], W=[('Sf', hv)])
            a.op('pool', lambda e, hv=hv: e.tensor_copy(out=Sb[:, hv * 128:(hv + 1) * 128], in_=Sf[:, hv * 128:(hv + 1) * 128]),
                 R=[('Sf', hv)], W=[('Sb', hv)])
        okeys = [('ps', 3)]
        nkeys = [('ps', 7)]
        a.op('act', lambda e: e.activation(out=sqo[:, :], in_=PS[3][:, :], func=AF.Square), R=okeys, W=['sqo'])
        a.op('pe', lambda e: e.matmul(PS[7][:, :], lhsT=onesb[:, :], rhs=sqo[:, :], start=True, stop=True), R=['sqo', 'consts'], W=nkeys)
        a.op('act', lambda e: e.activation(out=rso[:, :], in_=PS[7][:, :], func=AF.Sqrt, bias=epsb[:, :], scale=1.0 / 128), R=nkeys + ['consts'], W=['rso'])
        a.op('dve', lambda e: e.reciprocal(out=rso[:, :], in_=rso[:, :]), R=['rso'], W=['rso'])
        a.op('dve', lambda e: e.scalar_tensor_tensor(out=to[:, :], in0=PS[3][:, :], scalar=cfs[:, GAIN:GAIN + 1], in1=rso[:, :], op0=ALU.mult, op1=ALU.mult),
             R=okeys + ['rso', 'consts'], W=['to'])
        o = ocnt[0] % 2
        ocnt[0] += 1
        a.op('pool', lambda e, o=o: e.tensor_tensor(out=oo[o][:, :], in0=to[:, :], in1=sz[cb][:, gi * 512:(gi + 1) * 512], op=ALU.mult),
             R=['to', ('sz', cb)], W=[('oo', o)])
        a.dma('sp', oT[gi * 512:(gi + 1) * 512, n * CH:(n + 1) * CH].rearrange("(u p) c -> p u c", p=128),
              oo[o][:, :].rearrange("p (u c) -> p u c", c=128), R=[('oo', o)], sk=('oo', o))

    units = [(n, gi) for n in range(NCHK) for gi in range(NG)]
    for i, (n, gi) in enumerate(units):
        if gi == 0:
            chunk_prep(n)
        pre(n, gi, i % NSL)
        if i > 0:
            pn, pg = units[i - 1]
            rec(pn, pg, (i - 1) % NSL)
    pn, pg = units[-1]
    rec(pn, pg, (len(units) - 1) % NSL)
    return kb.close()


def e_consts(out_gain):
    cf = np.zeros((128, 385), np.float32)
    k = np.arange(128)[:, None]
    i = np.arange(128)[None, :]
    cf[:, 0:128] = (k <= i)
    cf[127, 128:256] = 1.0
    cf[:, 256:384] = np.eye(128)
    cf[:, 384] = out_gain
    cb = np.zeros((128, 512), np.float32)
    cb[:, 0:128] = np.eye(128)
    cb[:, 128:256] = np.where(k > i, 0.0, NEGB)
    cb[:, 256:384] = np.where(i > k, 0.0, NEGB)
    cb[:, 384:512] = np.where(i >= k, 0.0, NEGB)
    sel = np.zeros((16, 2, 16, 128), np.float32)
    for h in range(16):
        sel[h, 0, h, :] = 1.0
        sel[h, 1, h, :] = -1.0
    return cf, cb.astype(NPBF), sel.reshape(16, -1)


NCORE = 8
B, SEQ = 4, 4096
TH = SEQ // 2
_PROGS = {}
_DBG = None


def _prog(key, fn):
    if key not in _PROGS:
        _PROGS[key] = fn()
    return _PROGS[key]


def _run(nc, in_maps):
    res = run_bass_kernel_spmd(nc, in_maps, core_ids=list(range(NCORE)))
    return res.results


def _halo_T(full_T, s, halo):
    F = full_T.shape[0]
    out = np.zeros((F, halo + TH), full_T.dtype)
    lo = s * TH - halo
    if lo < 0:
        out[:, halo:] = full_T[:, 0:TH]
    else:
        out[:, :] = full_T[:, lo:lo + halo + TH]
    return out


def kernel(x, positions, mla_norm, mla_w_in, mla_q_norm, mla_kv_norm, mla_w_uq, mla_w_ukv,
           mla_w_o, gdn_norm, gdn_w_in, gdn_conv_w, gdn_a_log, gdn_dt_bias, gdn_out_norm,
           gdn_w_o, ffn_norm, ffn_w_up, ffn_conv_w, ffn_conv_b, ffn_w_down, final_norm):
    f32 = lambda t: np.ascontiguousarray(np.asarray(t, dtype=np.float32))
    x = f32(x)
    positions = np.asarray(positions).astype(np.int32)
    cores = [(c // 2, c % 2) for c in range(NCORE)]
    xT = [np.ascontiguousarray(x[b].T) for b in range(B)]

    wi, wq, wk, wv = a_weights(f32(mla_w_in[0]), f32(mla_w_uq[0]), f32(mla_w_ukv[0]))
    ca = a_consts(f32(mla_norm[0]), f32(mla_q_norm[0]), f32(mla_kv_norm[0]))
    ims = []
    for (b, s) in cores:
        ims.append({"xT": np.ascontiguousarray(xT[b][:, s * TH:(s + 1) * TH]),
                    "pos": np.ascontiguousarray(np.broadcast_to(positions[b, s * TH:(s + 1) * TH][None], (64, TH))),
                    "w_in": wi, "w_uq": wq, "w_uk": wk, "w_v": wv, "cst": ca})
    ra = _run(_prog('a', lambda: build_a(TH)), ims)
    cat = lambda b, name, ax: np.concatenate([ra[2 * b][name], ra[2 * b + 1][name]], axis=ax)
    msk = b_mask()
    ims = []
    for (b, g) in cores:
        hs = slice(8 * g, 8 * g + 8)
        V = cat(b, "V", 0).reshape(SEQ // 128, 128, NH, 128)[:, :, hs]
        ims.append({"qnT": np.ascontiguousarray(cat(b, "qnT", 2)[hs]), "qrT": np.ascontiguousarray(cat(b, "qrT", 2)[hs]),
                    "knT": np.ascontiguousarray(cat(b, "knT", 2)[hs]), "krT": np.ascontiguousarray(cat(b, "krT", 1)),
                    "V": np.ascontiguousarray(V.transpose(2, 1, 0, 3)), "msk": msk})
    rb = _run(_prog('b', lambda: build_b(SEQ, 8)), ims)
    oT0 = [np.concatenate([rb[2 * b]["oT"], rb[2 * b + 1]["oT"]], axis=0) for b in range(B)]

    def run_of(hT_full, oT_full, w_o, li, final, key):
        KO = w_o.shape[0]
        wo_t, wu_t, wd_t = tile_w(f32(w_o)), tile_w(f32(ffn_w_up[li])), tile_w(f32(ffn_w_down[li]))
        cst = of_consts(f32(ffn_norm[li]), f32(final_norm), f32(ffn_conv_w[li]), f32(ffn_conv_b[li]))
        ims = []
        for (b, s) in cores:
            ims.append({"hT": _halo_T(hT_full[b], s, HL), "oT": _halo_T(oT_full[b], s, HL),
                        "w_o": wo_t, "w_up": wu_t, "w_dn": wd_t, "cst": cst})
        r = _run(_prog(key, lambda: build_of(TH, KO, final)), ims)
        return [np.concatenate([r[2 * b]["outT"], r[2 * b + 1]["outT"]], axis=1) for b in range(B)]

    h2T = run_of(xT, oT0, mla_w_o[0], 0, False, 'of0')

    if _DBG is not None:
        _DBG['oT0'] = oT0[0]
        _DBG['h2T'] = h2T[0]
    wd_in = d_weights(f32(gdn_w_in[0]))
    cd = d_consts(f32(gdn_norm[0]), f32(gdn_conv_w[0]), f32(gdn_a_log[0]), f32(gdn_dt_bias[0]))
    ims = [{"hT": _halo_T(h2T[b], s, HD), "w_in": wd_in, "cst": cd} for (b, s) in cores]
    rd = _run(_prog('d', lambda: build_d(TH)), ims)
    catd = lambda b, name, ax: np.concatenate([rd[2 * b][name], rd[2 * b + 1][name]], axis=ax)
    cf, cb16, sel = e_consts(f32(gdn_out_norm[0]))
    NCHK = SEQ // CH
    ims = []
    for (b, g) in cores:
        qs = slice(8 * g, 8 * g + 8)
        vs = slice(16 * g, 16 * g + 16)
        q = catd(b, "qT", 2)[qs].reshape(8, 128, NCHK, CH)
        k = catd(b, "kT", 2)[qs].reshape(8, 128, NCHK, CH)
        v = catd(b, "vT", 2)[vs].reshape(16, 128, NCHK, CH)
        z = catd(b, "szT", 2)[vs].reshape(16, 128, NCHK, CH)
        gb = catd(b, "gbT", 1)
        gg = gb[32 + 16 * g:32 + 16 * g + 16].T.reshape(NCHK, CH, 16)
        lb = gb[16 * g:16 * g + 16].T.reshape(NCHK, CH, 16)
        ims.append({"qTc": np.ascontiguousarray(q.transpose(2, 1, 0, 3)).reshape(NCHK, 128, 8 * 128),
                    "kTc": np.ascontiguousarray(k.transpose(2, 1, 0, 3)).reshape(NCHK, 128, 8 * 128),
                    "ktm": np.ascontiguousarray(k.transpose(2, 3, 0, 1)).reshape(NCHK, 128, 8 * 128),
                    "vtm": np.ascontiguousarray(v.transpose(2, 3, 0, 1)).reshape(NCHK, 128, 16 * 128),
                    "szc": np.ascontiguousarray(z.transpose(2, 1, 0, 3)).reshape(NCHK, 128, 16 * 128),
                    "glb": np.ascontiguousarray(np.concatenate([gg, lb], axis=2)),
                    "cf": cf, "cb16": cb16, "selc": sel})
    re_ = _run(_prog('e', lambda: build_e(SEQ, 4)), ims)
    oT1 = [np.concatenate([re_[2 * b]["oT"], re_[2 * b + 1]["oT"]], axis=0) for b in range(B)]

    if _DBG is not None:
        _DBG['oT1'] = oT1[0]
        _DBG['rd'] = {k2: np.concatenate([rd[0][k2], rd[1][k2]], axis=-1) for k2 in rd[0]}
    outT = run_of(h2T, oT1, gdn_w_o[0], 1, True, 'of1')
    return np.ascontiguousarray(np.stack([o.T for o in outT], axis=0)).astype(np.float32)
```
